# Optimizing a Trainium2 kernel written in Bass

```python
import math
import jax, jax.numpy as jnp
from jax import lax
import numpy as np

D_MODEL = 2048
BATCH = 4
SEQ = 2048
DEPTH = 2

N_A_LAYERS = DEPTH // 2
N_B_LAYERS = DEPTH - N_A_LAYERS
N_SUBLAYERS = 3
FFN_RESIDUAL_WEIGHT = 0.5
D_FF = 5632
RMS_EPS = 1e-6
ADA_SCALE = 0.1

S5_GROUP = 16
N_S5_GROUPS = D_MODEL // S5_GROUP
S5_STATE = 64
DT_MIN = 1e-3
DT_MAX = 1e-1

HEAD_DIM = 64
N_Q_HEADS = D_MODEL // HEAD_DIM
N_KV_HEADS = N_Q_HEADS // 8
Q_PER_KV = N_Q_HEADS // N_KV_HEADS
WINDOW = 128
ATTN_BLOCK = 128
ROPE_THETA = 10000.0

kernel_name = "yoco_s5_swa_sink_macaron_adaln"


def _rmsnorm(x, g):
    xf = x.astype(jnp.float32)
    xf = xf * lax.rsqrt(jnp.mean(xf * xf, axis=-1, keepdims=True) + RMS_EPS)
    return (xf * g.astype(jnp.float32)).astype(x.dtype)


def _modulate(h, shift, scale):
    return h * (1.0 + scale[:, None, :]) + shift[:, None, :]


def _swiglu(h, w_in, w_out):
    gate, up = jnp.split(h @ w_in, 2, axis=-1)
    return (jax.nn.silu(gate) * up) @ w_out


def _rope_tables(positions):
    inv_freq = 1.0 / (ROPE_THETA ** (jnp.arange(0, HEAD_DIM, 2, dtype=jnp.float32) / HEAD_DIM))
    ang = positions.astype(jnp.float32)[..., None] * inv_freq
    return jnp.cos(ang)[:, :, None, :], jnp.sin(ang)[:, :, None, :]


def _apply_rope(t, cos, sin):
    tf = t.astype(jnp.float32)
    t1, t2 = jnp.split(tf, 2, axis=-1)
    return jnp.concatenate([t1 * cos - t2 * sin, t2 * cos + t1 * sin], axis=-1).astype(t.dtype)


def _s5_mixer(u, w_in, a_re, a_im, b_re, b_im, c_re, c_im, d_skip, log_dt, w_glu, b_glu, w_out):
    Bsz, L, D = u.shape
    v = (u @ w_in).reshape(Bsz, L, N_S5_GROUPS, S5_GROUP).astype(jnp.float32)
    lam = lax.complex(a_re.astype(jnp.float32), a_im.astype(jnp.float32))
    dt = jnp.exp(log_dt.astype(jnp.float32))[:, None]
    lam_bar = jnp.exp(lam * dt)
    b_mat = lax.complex(b_re.astype(jnp.float32), b_im.astype(jnp.float32))
    b_bar = ((lam_bar - 1.0) / lam)[..., None] * b_mat
    c_mat = lax.complex(c_re.astype(jnp.float32), c_im.astype(jnp.float32))
    bu = jnp.einsum('blgh,gph->blgp', v.astype(jnp.complex64), b_bar)
    a_elems = jnp.broadcast_to(lam_bar, (1, L) + lam_bar.shape)

    def combine(left, right):
        a_l, b_l = left
        a_r, b_r = right
        return a_r * a_l, a_r * b_l + b_r

    _, states = lax.associative_scan(combine, (a_elems, bu), axis=1)
    y = jnp.einsum('blgp,ghp->blgh', states, c_mat).real + d_skip.astype(jnp.float32) * v
    y = jax.nn.gelu(y.reshape(Bsz, L, D)).astype(u.dtype)
    y = y * jax.nn.sigmoid(y @ w_glu + b_glu)
    return y @ w_out


def _banded(t):
    Bsz, L = t.shape[:2]
    nb = L // ATTN_BLOCK
    cur = t.reshape(Bsz, nb, ATTN_BLOCK, t.shape[2], t.shape[3])
    prev = jnp.pad(cur[:, :-1], ((0, 0), (1, 0), (0, 0), (0, 0), (0, 0)))
    return jnp.concatenate([prev, cur], axis=2)


def _band_mask(nb):
    q_pos = jnp.arange(ATTN_BLOCK)[:, None] + ATTN_BLOCK
    k_pos = jnp.arange(2 * ATTN_BLOCK)[None, :]
    diff = q_pos - k_pos
    in_window = (diff >= 0) & (diff < WINDOW)
    k_abs = jnp.arange(nb)[:, None] * ATTN_BLOCK - ATTN_BLOCK + k_pos
    return in_window[None] & (k_abs >= 0)[:, None, :]


def _shared_kv(h, c_act, kv_norm_g, w_ada_kv, b_ada_kv, w_kv, cos, sin):
    Bsz, L, _ = h.shape
    shift, scale = jnp.split(c_act @ w_ada_kv + b_ada_kv, 2, axis=-1)
    hn = _modulate(_rmsnorm(h, kv_norm_g), shift, scale)
    k, v = jnp.split(hn @ w_kv, 2, axis=-1)
    k = _apply_rope(k.reshape(Bsz, L, N_KV_HEADS, HEAD_DIM), cos, sin)
    v = v.reshape(Bsz, L, N_KV_HEADS, HEAD_DIM)
    return _banded(k), _banded(v)


def _swa_sink_attention(h, k_band, v_band, mask, cos, sin, w_q, sinks, w_o):
    Bsz, L, _ = h.shape
    nb = L // ATTN_BLOCK
    q = _apply_rope((h @ w_q).reshape(Bsz, L, N_Q_HEADS, HEAD_DIM), cos, sin)
    q = q.reshape(Bsz, nb, ATTN_BLOCK, N_KV_HEADS, Q_PER_KV, HEAD_DIM)
    s = jnp.einsum('bnqhgd,bnkhd->bnhgqk', q, k_band,
                   preferred_element_type=jnp.float32) * (HEAD_DIM ** -0.5)
    s = jnp.where(mask[None, :, None, None], s, -jnp.inf)
    sink = jnp.broadcast_to(sinks.astype(jnp.float32).reshape(1, 1, N_KV_HEADS, Q_PER_KV, 1, 1),
                            s.shape[:-1] + (1,))
    p = jax.nn.softmax(jnp.concatenate([s, sink], axis=-1), axis=-1)[..., :-1]
    o = jnp.einsum('bnhgqk,bnkhd->bnqhgd', p.astype(v_band.dtype), v_band)
    return o.reshape(Bsz, L, N_Q_HEADS * HEAD_DIM) @ w_o


def setup_inputs(seed: int = 0) -> dict:
    key = jax.random.key(seed)
    ks = jax.random.split(key, 32)
    f32 = jnp.float32
    D, F, G, P, H = D_MODEL, D_FF, N_S5_GROUPS, S5_STATE, S5_GROUP
    kvw = N_KV_HEADS * HEAD_DIM
    qw = N_Q_HEADS * HEAD_DIM

    def nrm(k, shape, std):
        return jax.random.normal(k, shape, f32) * std

    x = jax.random.normal(ks[0], (BATCH, SEQ, D), f32)
    c = jax.random.normal(ks[1], (BATCH, D), f32)
    offsets = jax.random.randint(ks[2], (BATCH, 1), 0, 4096, dtype=jnp.int32)
    positions = offsets + jnp.arange(SEQ, dtype=jnp.int32)[None, :]

    norm_g = 1.0 + nrm(ks[3], (DEPTH, N_SUBLAYERS, D), 0.02)
    w_ada = nrm(ks[4], (DEPTH, D, N_SUBLAYERS * 3 * D), ADA_SCALE * D ** -0.5)
    b_ada = nrm(ks[5], (DEPTH, N_SUBLAYERS * 3 * D), 0.01)
    w_ff_in = nrm(ks[6], (DEPTH, 2, D, 2 * F), D ** -0.5)
    w_ff_out = nrm(ks[7], (DEPTH, 2, F, D), F ** -0.5)

    s5_w_in = nrm(ks[8], (N_A_LAYERS, D, D), D ** -0.5)
    n_idx = jnp.arange(P, dtype=f32)
    s5_a_re = -0.5 + nrm(ks[9], (N_A_LAYERS, G, P), 0.01)
    s5_a_im = math.pi * n_idx + nrm(ks[10], (N_A_LAYERS, G, P), 0.01)
    s5_b_re = nrm(ks[11], (N_A_LAYERS, G, P, H), H ** -0.5)
    s5_b_im = nrm(ks[12], (N_A_LAYERS, G, P, H), H ** -0.5)
    s5_c_re = nrm(ks[13], (N_A_LAYERS, G, H, P), P ** -0.5)
    s5_c_im = nrm(ks[14], (N_A_LAYERS, G, H, P), P ** -0.5)
    s5_d = nrm(ks[15], (N_A_LAYERS, G, H), 1.0)
    s5_log_dt = jax.random.uniform(ks[16], (N_A_LAYERS, G), f32,
                                   math.log(DT_MIN), math.log(DT_MAX))
    s5_w_glu = nrm(ks[17], (N_A_LAYERS, D, D), D ** -0.5)
    s5_b_glu = nrm(ks[18], (N_A_LAYERS, D), 0.01)
    s5_w_out = nrm(ks[19], (N_A_LAYERS, D, D), D ** -0.5)

    kv_norm_g = 1.0 + nrm(ks[20], (D,), 0.02)
    w_ada_kv = nrm(ks[21], (D, 2 * D), ADA_SCALE * D ** -0.5)
    b_ada_kv = nrm(ks[22], (2 * D,), 0.01)
    w_kv = nrm(ks[23], (D, 2 * kvw), D ** -0.5)

    attn_w_q = nrm(ks[24], (N_B_LAYERS, D, qw), D ** -0.5)
    attn_sinks = nrm(ks[25], (N_B_LAYERS, N_Q_HEADS), 1.0)
    attn_w_o = nrm(ks[26], (N_B_LAYERS, qw, D), qw ** -0.5)

    final_norm_g = 1.0 + nrm(ks[27], (D,), 0.02)

    return {"x": x, "c": c, "positions": positions,
            "norm_g": norm_g, "w_ada": w_ada, "b_ada": b_ada,
            "w_ff_in": w_ff_in, "w_ff_out": w_ff_out,
            "s5_w_in": s5_w_in, "s5_a_re": s5_a_re, "s5_a_im": s5_a_im,
            "s5_b_re": s5_b_re, "s5_b_im": s5_b_im, "s5_c_re": s5_c_re, "s5_c_im": s5_c_im,
            "s5_d": s5_d, "s5_log_dt": s5_log_dt, "s5_w_glu": s5_w_glu, "s5_b_glu": s5_b_glu,
            "s5_w_out": s5_w_out,
            "kv_norm_g": kv_norm_g, "w_ada_kv": w_ada_kv, "b_ada_kv": b_ada_kv, "w_kv": w_kv,
            "attn_w_q": attn_w_q, "attn_sinks": attn_sinks, "attn_w_o": attn_w_o,
            "final_norm_g": final_norm_g}


def reference(x, c, positions, norm_g, w_ada, b_ada, w_ff_in, w_ff_out,
              s5_w_in, s5_a_re, s5_a_im, s5_b_re, s5_b_im, s5_c_re, s5_c_im,
              s5_d, s5_log_dt, s5_w_glu, s5_b_glu, s5_w_out,
              kv_norm_g, w_ada_kv, b_ada_kv, w_kv,
              attn_w_q, attn_sinks, attn_w_o, final_norm_g):
    Bsz, L, D = x.shape
    c_act = jax.nn.silu(c)
    cos, sin = _rope_tables(positions)
    mask = _band_mask(L // ATTN_BLOCK)
    k_band = v_band = None
    for layer in range(DEPTH):
        mod = (c_act @ w_ada[layer] + b_ada[layer]).reshape(Bsz, N_SUBLAYERS, 3, D)
        shift, scale, gate = mod[:, :, 0], mod[:, :, 1], mod[:, :, 2]
        g = norm_g[layer]
        h = _modulate(_rmsnorm(x, g[0]), shift[:, 0], scale[:, 0])
        x = x + FFN_RESIDUAL_WEIGHT * (1.0 + gate[:, 0, None, :]) * _swiglu(
            h, w_ff_in[layer, 0], w_ff_out[layer, 0])
        h = _modulate(_rmsnorm(x, g[1]), shift[:, 1], scale[:, 1])
        if layer < N_A_LAYERS:
            i = layer
            y = _s5_mixer(h, s5_w_in[i], s5_a_re[i], s5_a_im[i], s5_b_re[i], s5_b_im[i],
                          s5_c_re[i], s5_c_im[i], s5_d[i], s5_log_dt[i],
                          s5_w_glu[i], s5_b_glu[i], s5_w_out[i])
        else:
            j = layer - N_A_LAYERS
            y = _swa_sink_attention(h, k_band, v_band, mask, cos, sin,
                                    attn_w_q[j], attn_sinks[j], attn_w_o[j])
        x = x + (1.0 + gate[:, 1, None, :]) * y
        h = _modulate(_rmsnorm(x, g[2]), shift[:, 2], scale[:, 2])
        x = x + FFN_RESIDUAL_WEIGHT * (1.0 + gate[:, 2, None, :]) * _swiglu(
            h, w_ff_in[layer, 1], w_ff_out[layer, 1])
        if layer == N_A_LAYERS - 1:
            k_band, v_band = _shared_kv(x, c_act, kv_norm_g, w_ada_kv, b_ada_kv, w_kv, cos, sin)
    return _rmsnorm(x, final_norm_g)
```

```python
import math
import numpy as np
import concourse.bass as bass
import concourse.mybir as mybir
from concourse.bass_utils import run_bass_kernel_spmd

F32 = mybir.dt.float32
BF16 = mybir.dt.bfloat16
I32 = mybir.dt.int32
AF = mybir.ActivationFunctionType
ALU = mybir.AluOpType

D = 2048
DC = 16
FF = 5632
T = 1024
NQ = 4
FQ = 11
EPS = 1e-6
ENGS = ("pe", "act", "dve", "pool", "sp")


class Op:
    __slots__ = ("eng", "fn", "deps", "signal", "semkey", "tok")

    def __init__(self, eng, fn, semkey=None):
        self.eng, self.fn, self.deps, self.signal, self.semkey, self.tok = eng, fn, [], False, semkey, None


class Prog:
    def __init__(self):
        self.ops = {e: [] for e in ENGS}
        self.regs = {}

    def add(self, eng, fn, reads=(), writes=(), semkey=None):
        op = Op(eng, fn, semkey)
        deps = []
        for r in reads:
            st = self.regs.setdefault(r, [None, {}])
            if st[0] is not None:
                deps.append(st[0])
        for r in writes:
            st = self.regs.setdefault(r, [None, {}])
            if st[0] is not None:
                deps.append(st[0])
            deps.extend(st[1].values())
        for r in reads:
            st = self.regs[r]
            key = eng if semkey is None else id(op)
            st[1][key] = op
        for r in writes:
            self.regs[r] = [op, {}]
        seen = set()
        for d in deps:
            if d is op or id(d) in seen:
                continue
            seen.add(id(d))
            if d.eng == "pe" and eng == "pe" and d.semkey is None and semkey is None:
                continue
            op.deps.append(d)
            d.signal = True
        self.ops[eng].append(op)
        return op

    def emit(self, nc, block_engs, sems):
        engsem = {e: sems.pop() for e in ENGS}
        keysem = {}
        keycnt = {}
        for e in ENGS:
            cnt = 0
            for op in self.ops[e]:
                if op.semkey is not None:
                    if op.semkey not in keysem:
                        keysem[op.semkey] = sems.pop()
                        keycnt[op.semkey] = 0
                    keycnt[op.semkey] += 16
                    op.tok = (keysem[op.semkey], keycnt[op.semkey])
                elif op.signal:
                    cnt += 1
                    op.tok = (engsem[e], cnt)

        def run(e, eng):
            known = {}
            for op in self.ops[e]:
                need = {}
                for d in op.deps:
                    s, c = d.tok
                    if known.get(id(s), 0) < c:
                        if id(s) not in need or need[id(s)][1] < c:
                            need[id(s)] = (s, c)
                for s, c in need.values():
                    eng.wait_ge(s, c)
                    known[id(s)] = c
                if op.fn is None:
                    continue
                ins = op.fn(eng)
                if op.semkey is not None:
                    ins.then_inc(op.tok[0], 16)
                elif op.signal:
                    ins.then_inc(op.tok[0], 1)
        return run


def build_program(debug_stage=None):
    nc = bass.Bass("TRN2", target_bir_lowering=False, dynamic_dma_scratch_size=8192)
    P = Prog()

    def dram(name, shape, dt=F32, kind="ExternalInput"):
        return nc.dram_tensor(name, list(shape), dt, kind=kind).ap()

    x_own = dram("x_own", [128, DC, T])
    x_prev = dram("x_prev", [128, DC, T])
    flag_d = dram("flag", [128, 1])
    c_bc = dram("c_bc", [128, D])
    norm_g = dram("norm_g", [128, 6 * DC])
    fin_g = dram("fin_g", [128, DC])
    kvn_g = dram("kvn_g", [128, DC])
    b_ada = dram("b_ada", [128, 2 * 144])
    b_adakv = dram("b_adakv", [128, 32])
    w_adaT = dram("w_adaT", [2, 144, 128, D])
    w_adakvT = dram("w_adakvT", [32, 128, D])
    w_ff_in = dram("w_ff_in", [2, 2, D, 2 * FF])
    w_ff_out = dram("w_ff_out", [2, 2, FF, D])
    out_d = dram("out", [128, DC, T], kind="ExternalOutput")

    ctx = []

    def sb(name, shape, dt=F32):
        t = nc.sbuf_tensor(name, list(shape), dt)
        ctx.append(t)
        return t.__enter__()

    xT = sb("xT", [128, DC, T])
    B1 = sb("B1", [128, DC, T], BF16)
    B2 = sb("B2", [128, DC, T], BF16)
    wring = sb("wring", [128, 3, 4096], BF16)
    modraw = sb("modraw", [128, 2 * 144 + 32])
    bias_sb = sb("bias_sb", [128, 2 * 144 + 32])
    ng_sb = sb("ng_sb", [128, 8 * DC])
    AB = sb("AB", [128, 3, DC])
    flag = sb("flagsb", [128, 1])
    ones_bf = sb("ones_bf", [128, 128], BF16)
    epst = sb("epst", [128, 1])
    U32 = sb("U32", [128, 12864])
    sg = U32[:, 0:1024].bitcast(BF16).rearrange("p (a b) -> p a b", a=2)
    psc = nc.psum_tensor("ps", [128, 8, 512], F32)
    ctx.append(psc)
    ps = psc.__enter__()
    B2f = B2[:].rearrange("p a b -> p (a b)").bitcast(F32)

    wslot = [0]

    def load_w(dst_elems_ap_fn, src_ap, reads=()):
        s = wslot[0] % 3
        wslot[0] += 1
        P.add("pool", lambda g, s=s: g.dma_start(out=dst_elems_ap_fn(wring[:, s, :]), in_=src_ap),
              reads=reads, writes=[("w", s)], semkey=("w", s))
        return s

    aslot = [0]

    def ada_chunk(src_ap, col):
        s = aslot[0] % 4
        aslot[0] += 1
        P.add("sp", lambda g: g.dma_start(out=aring[:, s, :], in_=src_ap), writes=[("a", s)], semkey=("a", s))
        P.add("dve", lambda v: v.scalar_tensor_tensor(out=aring[:, s, :], in0=aring[:, s, :], scalar=1.0,
                                                      in1=cact[:], op0=ALU.mult, op1=ALU.mult,
                                                      accum_out=modraw[:, col:col + 1]),
              reads=[("a", s), "cact"], writes=[("a", s), ("mod", col)])

    ada_jobs = []

    def queue_ada(layer, sub):
        for kind in range(3):
            for dc in range(DC):
                j = (sub * 3 + kind) * DC + dc
                ada_jobs.append((w_adaT[layer, j], layer * 144 + j))

    def queue_ada_kv():
        for j in range(32):
            ada_jobs.append((w_adakvT[j], 288 + j))

    def drain_ada(n):
        for _ in range(min(n, len(ada_jobs))):
            src, col = ada_jobs.pop(0)
            ada_chunk(src, col)

    def mod_cols(base):
        return [("mod", base + i) for i in range(DC)]

    def prep_AB(gcol, base, has_gate=True, gate_scale=0.5):
        rd = mod_cols(base) + mod_cols(base + DC) + (mod_cols(base + 2 * DC) if has_gate else []) + ["bias", "ng"]
        P.add("dve", lambda v: v.tensor_tensor(out=AB[:, 1, :], in0=modraw[:, base:base + DC],
                                               in1=bias_sb[:, base:base + DC], op=ALU.add),
              reads=rd, writes=["AB1"])
        P.add("dve", lambda v: v.tensor_tensor(out=AB[:, 0, :], in0=modraw[:, base + DC:base + 2 * DC],
                                               in1=bias_sb[:, base + DC:base + 2 * DC], op=ALU.add),
              reads=rd, writes=["AB0"])
        P.add("dve", lambda v: v.scalar_tensor_tensor(out=AB[:, 0, :], in0=AB[:, 0, :], scalar=1.0,
                                                      in1=ng_sb[:, gcol:gcol + DC], op0=ALU.add, op1=ALU.mult),
              reads=["AB0", "ng"], writes=["AB0"])
        if has_gate:
            P.add("dve", lambda v: v.tensor_tensor(out=AB[:, 2, :], in0=modraw[:, base + 2 * DC:base + 3 * DC],
                                                   in1=bias_sb[:, base + 2 * DC:base + 3 * DC], op=ALU.add),
                  reads=rd, writes=["AB2"])
            P.add("dve", lambda v: v.tensor_scalar(out=AB[:, 2, :], in0=AB[:, 2, :], scalar1=1.0,
                                                   scalar2=gate_scale, op0=ALU.add, op1=ALU.mult),
                  reads=["AB2"], writes=["AB2"])

    XR = [("x", dc) for dc in range(DC)]

    def rmsnorm_mod(out_fp32_dst=None):
        for dc in range(DC):
            P.add("act", lambda a, dc=dc: a.activation(out=B1[:, dc, :], in_=xT[:, dc, :], func=AF.Square),
                  reads=[("x", dc)], writes=["B1"])
        for h in range(2):
            for dc in range(DC):
                P.add("pe", lambda t, dc=dc, h=h: t.matmul(ps[:, h, :], ones_bf[:], B1[:, dc, h * 512:(h + 1) * 512],
                                                           start=(dc == 0), stop=(dc == DC - 1)),
                      reads=["B1", "ones"], writes=[("ps", h)])
        rs = B2f[:, 0:T]
        tmp = [B2f[:, T:2 * T], B2f[:, 2 * T:3 * T]]
        P.add("act", lambda a: a.activation(out=rs, in_=ps[:, 0:2, :].rearrange("p a b -> p (a b)"), func=AF.Sqrt,
                                            scale=1.0 / D, bias=epst[:]),
              reads=[("ps", 0), ("ps", 1), "eps"], writes=["B2"])
        P.add("dve", lambda v: v.reciprocal(out=rs, in_=rs), reads=["B2"], writes=["B2"])
        for dc in range(DC):
            tm = tmp[dc % 2]
            P.add("dve", lambda v, dc=dc, tm=tm: v.scalar_tensor_tensor(out=tm, in0=xT[:, dc, :],
                                                                       scalar=AB[:, 0, dc:dc + 1], in1=rs,
                                                                       op0=ALU.mult, op1=ALU.mult),
                  reads=[("x", dc), "B2", "AB0"], writes=[("tmp", dc % 2)])
            if out_fp32_dst is None:
                P.add("act", lambda a, dc=dc, tm=tm: a.activation(out=B1[:, dc, :], in_=tm, func=AF.Identity,
                                                                   bias=AB[:, 1, dc:dc + 1], scale=1.0),
                      reads=[("tmp", dc % 2), "AB1"], writes=["B1"])
            else:
                P.add("act", lambda a, dc=dc, tm=tm: a.activation(out=out_fp32_dst[:, dc, :], in_=tm, func=AF.Copy),
                      reads=[("tmp", dc % 2)], writes=[("x", dc)])

    def ffn(layer, sub_idx):
        win = w_ff_in[layer, sub_idx]
        wout = w_ff_out[layer, sub_idx]
        jobs = []
        for q in range(NQ):
            for f in range(FQ):
                jobs.append(("in", q, f))
            for dc in range(DC):
                jobs.append(("out", q, dc))
        slots = {}

        def issue(i):
            if i >= len(jobs):
                return
            kind, q, a = jobs[i]
            if kind == "in":
                fcol = (q * FQ + a) * 128
                for gu in range(2):
                    src = win[:, gu * FF + fcol: gu * FF + fcol + 128].rearrange("(kc p) c -> p kc c", p=128)
                    if gu == 0:
                        s = wslot[0] % 3
                        wslot[0] += 1
                    P.add("pool", lambda g, s=s, gu=gu, src=src: g.dma_start(
                        out=wring[:, s, gu * 2048:(gu + 1) * 2048].rearrange("p (kc c) -> p kc c", c=128), in_=src),
                        writes=[("w", s, gu)], semkey=("w", s, gu))
                slots[i] = s
            else:
                src = wout[q * FQ * 128:(q + 1) * FQ * 128, a * 128:(a + 1) * 128].rearrange("(fc p) c -> p fc c", p=128)
                s = wslot[0] % 3
                wslot[0] += 1
                P.add("pool", lambda g, s=s, src=src: g.dma_start(
                    out=wring[:, s, 0:FQ * 128].rearrange("p (fc c) -> p fc c", c=128), in_=src),
                    writes=[("w", s, 0), ("w", s, 1)], semkey=("w", s, 0))
                slots[i] = s

        issue(0)
        issue(1)
        pb = [0]
        for i, (kind, q, a) in enumerate(jobs):
            issue(i + 2)
            s = slots[i]
            if kind == "in":
                base = 4 * (pb[0] % 2)
                pb[0] += 1
                for gu in range(2):
                    for kc in range(DC):
                        for h in range(2):
                            P.add("pe", lambda t, s=s, gu=gu, kc=kc, h=h, base=base: t.matmul(
                                ps[:, base + 2 * gu + h, :],
                                wring[:, s, gu * 2048 + kc * 128: gu * 2048 + (kc + 1) * 128],
                                B1[:, kc, h * 512:(h + 1) * 512], start=(kc == 0), stop=(kc == DC - 1)),
                                reads=["B1", ("w", s, gu)], writes=[("ps", base + 2 * gu + h)])
                sgi = a % 2
                P.add("act", lambda ac, base=base, sgi=sgi: ac.activation(
                    out=sg[:, sgi, :], in_=ps[:, base:base + 2, :].rearrange("p a b -> p (a b)"), func=AF.Silu),
                    reads=[("ps", base), ("ps", base + 1)], writes=[("sg", sgi)])
                P.add("dve", lambda v, base=base, sgi=sgi, a=a: v.tensor_tensor(
                    out=B2[:, a, :], in0=ps[:, base + 2:base + 4, :].rearrange("p a b -> p (a b)"), in1=sg[:, sgi, :],
                    op=ALU.mult),
                    reads=[("ps", base + 2), ("ps", base + 3), ("sg", sgi)], writes=[("B2", a)])
                drain_ada(1)
            else:
                base = 2 * (pb[0] % 4)
                pb[0] += 1
                for fc in range(FQ):
                    for h in range(2):
                        P.add("pe", lambda t, s=s, fc=fc, h=h, base=base: t.matmul(
                            ps[:, base + h, :], wring[:, s, fc * 128:(fc + 1) * 128],
                            B2[:, fc, h * 512:(h + 1) * 512], start=(fc == 0), stop=(fc == FQ - 1)),
                            reads=[("B2", fc), ("w", s, 0)], writes=[("ps", base + h)])
                P.add("dve", lambda v, base=base, a=a: v.scalar_tensor_tensor(
                    out=xT[:, a, :], in0=ps[:, base:base + 2, :].rearrange("p a b -> p (a b)"),
                    scalar=AB[:, 2, a:a + 1], in1=xT[:, a, :], op0=ALU.mult, op1=ALU.add),
                    reads=[("ps", base), ("ps", base + 1), ("x", a), "AB2"], writes=[("x", a)])
                if a % 2 == 0:
                    drain_ada(1)

    s5_w_in = dram("s5_w_in", [D, D])
    s5_w_glu = dram("s5_w_glu", [D, D])
    s5_w_out = dram("s5_w_out", [D, D])
    s5cm = dram("s5cm", [16, 128, 320])
    s5pm = dram("s5pm", [16, 128, 268])
    s5dv = dram("s5dv", [128, 2 * DC])
    cst = dram("cst", [128, 80])
    trimask = dram("trimask", [128, 2, 128])
    pos_d = dram("pos", [2, 128, T], I32)
    w_q2 = dram("w_q2", [2, D, D])
    w_k2 = dram("w_k2", [2, D, 1024])
    w_v = dram("w_v", [D, 256])
    w_o = dram("w_o", [D, D])
    sinks_d = dram("sinks", [128, 32])

    cstt = sb("cstt", [128, 80])
    dvt = sb("dvt", [128, 2 * DC])
    carry = sb("carry", [128, 2, 64])
    kh = sb("kh", [128, 4, 2, 128], BF16)
    vh = sb("vh", [128, 256], BF16)
    esink = sb("esink", [128, 32])
    maskb = sb("maskb", [128, 2, 128], BF16)
    maskh = sb("maskh", [128, 128], BF16)
    halfpi = sb("halfpi", [128, 1])
    KK_CM = cstt[:, 0:8]
    KK_PM = cstt[:, 8:17]
    MM4 = cstt[:, 17:21]
    MK2 = cstt[:, 21:37]
    MKB = cstt[:, 37:69]
    INVF = cstt[:, 69:70]
    SGN = cstt[:, 70:71]
    TWO_PI = 2.0 * math.pi
    PI_S = 3.1415925

    def V(fn, r, w):
        return P.add("dve", fn, reads=r, writes=w)

    def A(fn, r, w):
        return P.add("act", fn, reads=r, writes=w)

    def M(fn, r, w):
        return P.add("pe", fn, reads=r, writes=w)

    uoff = [1024]

    def carve(n, dt=F32, shape=None):
        w = n if dt == F32 else (n + 1) // 2
        a = U32[:, uoff[0]:uoff[0] + w]
        uoff[0] += w
        assert uoff[0] <= 12864, uoff[0]
        if dt != F32:
            a = a.bitcast(dt)
        if shape is not None:
            names = "abcd"[:len(shape)]
            kw = {names[i]: shape[i] for i in range(len(shape))}
            a = a.rearrange("p (" + " ".join(names) + ") -> p " + " ".join(names), **kw)
        return a

    bar_t = sb("bar_t", [128, 1])

    def full_barrier():
        keys = [k for k in P.regs.keys()]
        P.add("dve", lambda v: v.memset(bar_t[:], 0.0), reads=[], writes=keys + ["Ubar"])

    def cossin(nm, ang, I, Fa, S, C):
        V(lambda v: v.tensor_scalar(out=I, in0=ang, scalar1=1.0 / TWO_PI, scalar2=None, op0=ALU.mult), [nm + "A"], [nm + "I"])
        V(lambda v: v.tensor_copy(out=Fa, in_=I), [nm + "I"], [nm + "F"])
        V(lambda v: v.scalar_tensor_tensor(out=ang, in0=Fa, scalar=-TWO_PI, in1=ang, op0=ALU.mult, op1=ALU.add),
          [nm + "F", nm + "A"], [nm + "A"])
        V(lambda v: v.tensor_scalar(out=ang, in0=ang, scalar1=-PI_S, scalar2=PI_S, op0=ALU.max, op1=ALU.min), [nm + "A"], [nm + "A"])
        A(lambda a: a.activation(out=S, in_=ang, func=AF.Sin), [nm + "A"], [nm + "S"])
        A(lambda a: a.activation(out=Fa, in_=ang, func=AF.Abs), [nm + "A"], [nm + "F"])
        A(lambda a: a.activation(out=C, in_=Fa, func=AF.Sin, scale=-1.0, bias=halfpi[:]), [nm + "F", "halfpi"], [nm + "C"])

    def tt(out, a, b, op, r, w):
        V(lambda v: v.tensor_tensor(out=out, in0=a, in1=b, op=op), r, w)

    def dense(w2d, n_oc, evac, slots, bank_of, rhs, rreg, pre=None):
        tiles = [(s, g) for s in slots for g in range(2)]
        nt = len(tiles)

        def issue(i):
            if i >= n_oc:
                return
            s, g = tiles[i % nt]
            src = w2d[:, i * 128:(i + 1) * 128].rearrange("(kc p) c -> p kc c", p=128)
            P.add("pool", lambda e, s=s, g=g, src=src: e.dma_start(
                out=wring[:, s, g * 2048:(g + 1) * 2048].rearrange("p (kc c) -> p kc c", c=128), in_=src),
                writes=[("w", s, g)], semkey=("w", s, g))
        for i in range(3):
            issue(i)
        for oc in range(n_oc):
            issue(oc + 3)
            if pre is not None:
                pre(oc)
            s, g = tiles[oc % nt]
            base = bank_of(oc)
            for kc in range(DC):
                for h in range(2):
                    M(lambda t, s=s, g=g, kc=kc, h=h, base=base: t.matmul(
                        ps[:, base + h, :], wring[:, s, g * 2048 + kc * 128: g * 2048 + (kc + 1) * 128],
                        rhs(kc, h), start=(kc == 0), stop=(kc == DC - 1)),
                        [rreg(kc), ("w", s, g)], [("ps", base + h)])
            evac(oc, base)

    def rhs_B1(kc, h):
        return B1[:, kc, h * 512:(h + 1) * 512]

    def rhs_B2(kc, h):
        return B2[:, kc, h * 512:(h + 1) * 512]

    def resid_evac(oc, base):
        V(lambda v: v.scalar_tensor_tensor(out=xT[:, oc, :], in0=ps[:, base:base + 2, :].rearrange("p a b -> p (a b)"),
                                           scalar=AB[:, 2, oc:oc + 1], in1=xT[:, oc, :], op0=ALU.mult, op1=ALU.add),
          [("ps", base), ("ps", base + 1), ("x", oc), "AB2"], [("x", oc)])

    def s5_mixer(first_pass):
        full_barrier()
        uoff[0] = 1024
        vbf = carve(T, BF16)
        pcm = [carve(320)]
        ppm = [carve(268)]
        c_ = {k: carve(64) for k in ("dt", "base", "ang", "lr", "li", "t1", "t2", "t3", "cr", "ci", "bbr", "bbi", "F", "S", "C")}
        c_I = carve(64).bitcast(I32)
        pw = {k: carve(256, F32, (4, 64)) for k in ("A", "F", "S", "C", "E", "T1")}
        pw_I = pw["E"].rearrange("p a b -> p (a b)").bitcast(I32).rearrange("p (a b) -> p a b", a=4)
        pw["T2"] = pw["A"]
        BsT = wring[:, 2, :].rearrange("p (s r q c) -> p s r q c", s=8, r=2, q=4)
        m_ = {k: carve(4) for k in ("dt", "base", "ang", "lr", "li", "t1", "t2", "t3", "cr", "ci", "F", "S", "C")}
        m_I = carve(4).bitcast(I32)
        mbb = [carve(64, F32, (4, 16)), carve(64, F32, (4, 16))]
        mp = {k: carve(36, F32, (4, 9)) for k in ("A", "F", "S", "C", "E")}
        mp_I = carve(36).bitcast(I32).rearrange("p (a b) -> p a b", a=4)
        A01 = [carve(576, F32, (4, 9, 16)), carve(576, F32, (4, 9, 16))]
        At = carve(576, F32, (4, 9, 16))
        CRh = carve(4 * 9 * 2 * 64, BF16, (4, 9, 2, 64))
        Bfull = carve(4 * 2 * 128, BF16, (4, 2, 128))
        Kblk = carve(8 * 128, BF16, (8, 128))
        pl = [[carve(512, F32, (4, 128)) for _ in range(2)] for _ in range(2)]
        Hprev = carve(4 * 2 * 128, BF16, (4, 2, 128))
        Lp = [carve(28, F32, (7, 4)) for _ in range(3)]
        hi_ = [carve(4), carve(4), carve(4), carve(4)]
        yt = sg[:, 0, :].bitcast(F32)[:, 0:512]
        gt = sg[:, 1, :].bitcast(F32)[:, 0:512]

        def pre(cc):
            sl = 0
            P.add("sp", lambda e: e.dma_start(out=pcm[sl], in_=s5cm[cc]), reads=["Ubar"], writes=[("pcm", sl)], semkey=("pcm", sl))
            P.add("sp", lambda e: e.dma_start(out=ppm[sl], in_=s5pm[cc]), reads=["Ubar"], writes=[("ppm", sl)], semkey=("ppm", sl))

        def lam_chain(nm, d, I, are, aim, ldt, preg):
            A(lambda a: a.activation(out=d["dt"], in_=ldt, func=AF.Exp), [preg], [nm + "dt"])
            tt(d["base"], are, d["dt"], ALU.mult, [preg, nm + "dt"], [nm + "base"])
            tt(d["ang"], aim, d["dt"], ALU.mult, [preg, nm + "dt"], [nm + "A"])
            V(lambda v: v.tensor_copy(out=d["t3"], in_=d["ang"]), [nm + "A"], [nm + "ang0"])
            cossin(nm, d["ang"], I, d["F"], d["S"], d["C"])
            A(lambda a: a.activation(out=d["t1"], in_=d["base"], func=AF.Exp), [nm + "base"], [nm + "t1"])
            tt(d["lr"], d["t1"], d["C"], ALU.mult, [nm + "t1", nm + "C"], [nm + "lr"])
            tt(d["li"], d["t1"], d["S"], ALU.mult, [nm + "t1", nm + "S"], [nm + "li"])
            V(lambda v: v.tensor_scalar(out=d["t1"], in0=d["lr"], scalar1=-1.0, scalar2=None, op0=ALU.add), [nm + "lr"], [nm + "t1"])
            tt(d["t2"], are, are, ALU.mult, [preg], [nm + "t2"])
            tt(d["F"], aim, aim, ALU.mult, [preg], [nm + "F"])
            tt(d["t2"], d["t2"], d["F"], ALU.add, [nm + "t2", nm + "F"], [nm + "t2"])
            V(lambda v: v.reciprocal(out=d["t2"], in_=d["t2"]), [nm + "t2"], [nm + "t2"])
            tt(d["cr"], d["t1"], are, ALU.mult, [nm + "t1", preg], [nm + "cr"])
            tt(d["F"], d["li"], aim, ALU.mult, [nm + "li", preg], [nm + "F"])
            tt(d["cr"], d["cr"], d["F"], ALU.add, [nm + "cr", nm + "F"], [nm + "cr"])
            tt(d["cr"], d["cr"], d["t2"], ALU.mult, [nm + "cr", nm + "t2"], [nm + "cr"])
            tt(d["ci"], d["li"], are, ALU.mult, [nm + "li", preg], [nm + "ci"])
            tt(d["F"], d["t1"], aim, ALU.mult, [nm + "t1", preg], [nm + "F"])
            tt(d["ci"], d["ci"], d["F"], ALU.subtract, [nm + "ci", nm + "F"], [nm + "ci"])
            tt(d["ci"], d["ci"], d["t2"], ALU.mult, [nm + "ci", nm + "t2"], [nm + "ci"])

        def evac(cc, base):
            sl = 0
            cm, pm = pcm[sl], ppm[sl]
            rc, rp = ("pcm", sl), ("ppm", sl)
            A(lambda a: a.activation(out=vbf, in_=ps[:, 0:2, :].rearrange("p a b -> p (a b)"), func=AF.Copy),
              [("ps", 0), ("ps", 1)], ["vbf"])
            lam_chain("c", c_, c_I, cm[:, 0:64], cm[:, 64:128], cm[:, 128:192], rc)
            br, bi = cm[:, 192:256], cm[:, 256:320]
            tt(c_["bbr"], c_["cr"], br, ALU.mult, ["ccr", rc], ["cbbr"])
            tt(c_["F"], c_["ci"], bi, ALU.mult, ["cci", rc], ["cF"])
            tt(c_["bbr"], c_["bbr"], c_["F"], ALU.subtract, ["cbbr", "cF"], ["cbbr"])
            tt(c_["bbi"], c_["cr"], bi, ALU.mult, ["ccr", rc], ["cbbi"])
            tt(c_["F"], c_["ci"], br, ALU.mult, ["cci", rc], ["cF"])
            tt(c_["bbi"], c_["bbi"], c_["F"], ALU.add, ["cbbi", "cF"], ["cbbi"])
            for sh in range(2):
                kk3 = KK_CM[:, 4 * sh:4 * sh + 4].unsqueeze(2).to_broadcast([128, 4, 64])

                def b8(ap):
                    return ap.unsqueeze(1).to_broadcast([128, 4, 64])
                tt(pw["A"], kk3, b8(c_["t3"]), ALU.mult, ["cst", "cang0"], ["pA"])
                V(lambda v: v.tensor_scalar(out=pw_I, in0=pw["A"], scalar1=1.0 / TWO_PI, scalar2=None, op0=ALU.mult), ["pA"], ["pE"])
                V(lambda v: v.tensor_copy(out=pw["F"], in_=pw_I), ["pE"], ["pF"])
                V(lambda v: v.scalar_tensor_tensor(out=pw["A"], in0=pw["F"], scalar=-TWO_PI, in1=pw["A"], op0=ALU.mult, op1=ALU.add),
                  ["pF", "pA"], ["pA"])
                V(lambda v: v.tensor_scalar(out=pw["A"], in0=pw["A"], scalar1=-PI_S, scalar2=PI_S, op0=ALU.max, op1=ALU.min), ["pA"], ["pA"])
                A(lambda a: a.activation(out=pw["S"], in_=pw["A"], func=AF.Sin), ["pA"], ["pS"])
                A(lambda a: a.activation(out=pw["F"], in_=pw["A"], func=AF.Abs), ["pA"], ["pF"])
                A(lambda a: a.activation(out=pw["C"], in_=pw["F"], func=AF.Sin, scale=-1.0, bias=halfpi[:]), ["pF", "halfpi"], ["pC"])
                tt(pw["E"], kk3, b8(c_["base"]), ALU.mult, ["cst", "cbase"], ["pE"])
                A(lambda a: a.activation(out=pw["E"], in_=pw["E"], func=AF.Exp), ["pE"], ["pE"])
                tt(pw["C"], pw["C"], pw["E"], ALU.mult, ["pC", "pE"], ["pC"])
                tt(pw["S"], pw["S"], pw["E"], ALU.mult, ["pS", "pE"], ["pS"])
                tt(pw["T1"], pw["C"], b8(c_["bbr"]), ALU.mult, ["pC", "cbbr"], ["pT1"])
                tt(pw["F"], pw["S"], b8(c_["bbi"]), ALU.mult, ["pS", "cbbi"], ["pF"])
                tt(pw["T1"], pw["T1"], pw["F"], ALU.subtract, ["pT1", "pF"], ["pT1"])
                tt(pw["T2"], pw["C"], b8(c_["bbi"]), ALU.mult, ["pC", "cbbi"], ["pA"])
                tt(pw["F"], pw["S"], b8(c_["bbr"]), ALU.mult, ["pS", "cbbr"], ["pF"])
                tt(pw["T2"], pw["T2"], pw["F"], ALU.add, ["pA", "pF"], ["pA"])
                for ri, src, rg in ((0, pw["T1"], "pT1"), (1, pw["T2"], "pA")):
                    for s4 in range(4):
                        V(lambda v, ri=ri, src=src, s4=s4, sh=sh: v.tensor_tensor(
                            out=BsT[:, 4 * sh + s4, ri, :, :], in0=src[:, s4, :].unsqueeze(1).to_broadcast([128, 4, 64]),
                            in1=MM4.unsqueeze(2).to_broadcast([128, 4, 64]), op=ALU.mult),
                          [rg, "cst"], [("w", 2, 0), ("w", 2, 1)])
            for pr in range(4):
                hh, pp = pr // 2, pr % 2
                for ri in range(2):
                    for s in range(8):
                        M(lambda t, pr=pr, hh=hh, pp=pp, ri=ri, s=s: t.matmul(
                            ps[:, 2 + pr // 2, (pr % 2) * 256 + ri * 128:(pr % 2) * 256 + (ri + 1) * 128],
                            BsT[64 * hh:64 * hh + 64, s, ri, 2 * pp:2 * pp + 2, :].rearrange("p a b -> p (a b)"),
                            vbf[64 * hh:64 * hh + 64, s::8], start=(s == 0), stop=(s == 7), skip_group_check=True),
                          ["vbf", ("w", 2, 0), ("w", 2, 1)], [("ps", 2 + pr // 2)])
            lam_chain("m", m_, m_I, pm[:, 0:4], pm[:, 4:8], pm[:, 8:12], rp)
            crp = pm[:, 12:76].rearrange("p (a b) -> p a b", a=4)
            cip = pm[:, 76:140].rearrange("p (a b) -> p a b", a=4)
            brp = pm[:, 140:204].rearrange("p (a b) -> p a b", a=4)
            bip = pm[:, 204:268].rearrange("p (a b) -> p a b", a=4)

            def b16(ap):
                return ap.unsqueeze(2).to_broadcast([128, 4, 16])
            tt(mbb[0], b16(m_["cr"]), brp, ALU.mult, ["mcr", rp], ["mbb0"])
            tt(At[:, :, 0, :], b16(m_["ci"]), bip, ALU.mult, ["mci", rp], ["At"])
            tt(mbb[0], mbb[0], At[:, :, 0, :], ALU.subtract, ["mbb0", "At"], ["mbb0"])
            tt(mbb[1], b16(m_["cr"]), bip, ALU.mult, ["mcr", rp], ["mbb1"])
            tt(At[:, :, 0, :], b16(m_["ci"]), brp, ALU.mult, ["mci", rp], ["At"])
            tt(mbb[1], mbb[1], At[:, :, 0, :], ALU.add, ["mbb1", "At"], ["mbb1"])
            for pr in range(4):
                for pln in range(2):
                    V(lambda v, pr=pr, pln=pln: v.tensor_tensor(
                        out=Bfull[:, pr, pln, :].rearrange("p (a b) -> p a b", a=8),
                        in0=mbb[pln][:, pr, :].unsqueeze(1).to_broadcast([128, 8, 16]),
                        in1=MKB[:, pr * 8:(pr + 1) * 8].unsqueeze(2).to_broadcast([128, 8, 16]), op=ALU.mult),
                      ["mbb%d" % pln, "cst"], ["Bfull"])
            kk9 = KK_PM.unsqueeze(1).to_broadcast([128, 4, 9])

            def b9(ap):
                return ap.unsqueeze(2).to_broadcast([128, 4, 9])
            tt(mp["A"], kk9, b9(m_["t3"]), ALU.mult, ["cst", "mang0"], ["qA"])
            tt(mp["E"], kk9, b9(m_["base"]), ALU.mult, ["cst", "mbase"], ["qE"])
            cossin("q", mp["A"], mp_I, mp["F"], mp["S"], mp["C"])
            A(lambda a: a.activation(out=mp["E"], in_=mp["E"], func=AF.Exp), ["qE"], ["qE"])
            tt(mp["C"], mp["C"], mp["E"], ALU.mult, ["qC", "qE"], ["qC"])
            tt(mp["S"], mp["S"], mp["E"], ALU.mult, ["qS", "qE"], ["qS"])

            def bk(ap):
                return ap.unsqueeze(3).to_broadcast([128, 4, 9, 16])

            def bc(ap):
                return ap.unsqueeze(2).to_broadcast([128, 4, 9, 16])
            tt(A01[0], bc(crp), bk(mp["C"]), ALU.mult, [rp, "qC"], ["A0"])
            tt(At, bc(cip), bk(mp["S"]), ALU.mult, [rp, "qS"], ["At"])
            tt(A01[0], A01[0], At, ALU.subtract, ["A0", "At"], ["A0"])
            tt(A01[1], bc(crp), bk(mp["S"]), ALU.mult, [rp, "qS"], ["A1"])
            tt(At, bc(cip), bk(mp["C"]), ALU.mult, [rp, "qC"], ["At"])
            V(lambda v: v.scalar_tensor_tensor(out=A01[1], in0=A01[1], scalar=-1.0, in1=At, op0=ALU.mult, op1=ALU.subtract),
              ["A1", "At"], ["A1"])
            for pr in range(4):
                for pln in range(2):
                    V(lambda v, pr=pr, pln=pln: v.tensor_tensor(
                        out=CRh[:, pr, :, pln, :].rearrange("p k (q h) -> p k q h", q=4),
                        in0=A01[pln][:, pr, :, :].unsqueeze(2).to_broadcast([128, 9, 4, 16]),
                        in1=MK2[:, pr * 4:(pr + 1) * 4].unsqueeze(1).unsqueeze(3).to_broadcast([128, 9, 4, 16]), op=ALU.mult),
                      ["A%d" % pln, "cst"], ["CRh"])
            for hh in range(2):
                first = True
                for pq in range(2):
                    pr = 2 * hh + pq
                    for pln in range(2):
                        M(lambda t, hh=hh, pr=pr, pln=pln, first=first, last=(pq == 1 and pln == 1): t.matmul(
                            ps[:, 4 + hh, :].rearrange("p (k c) -> p k c", k=8), Bfull[:, pr, pln, :],
                            CRh[:, pr, 0:8, pln, :], start=first, stop=last, skip_group_check=True),
                          ["Bfull", "CRh"], [("ps", 4 + hh)])
                        first = False
                V(lambda v, hh=hh: v.tensor_copy(out=Kblk[:, :, 64 * hh:64 * hh + 64],
                                                 in_=ps[:, 4 + hh, :].rearrange("p (k c) -> p k c", k=8)),
                  [("ps", 4 + hh)], ["Kblk"])
            cur = pl[0]
            for ri in range(2):
                A(lambda a, ri=ri: a.activation(
                    out=pl[0][ri], in_=ps[:, 2:4, :].rearrange("p a (q r c) -> p (a q) r c", q=2, r=2)[:, :, ri, :], func=AF.Copy),
                  [("ps", 2), ("ps", 3)], [("pl", 0, ri)])
            cr4 = carry[:, :, 4 * cc:4 * cc + 4]
            if not first_pass:
                P8r, P8i = mp["C"][:, :, 8], mp["S"][:, :, 8]
                tt(hi_[0], P8r, cr4[:, 0, :], ALU.mult, ["qC", "carry"], ["hi0"])
                tt(hi_[1], P8i, cr4[:, 1, :], ALU.mult, ["qS", "carry"], ["hi1"])
                tt(hi_[0], hi_[0], hi_[1], ALU.subtract, ["hi0", "hi1"], ["hi0"])
                tt(hi_[2], P8r, cr4[:, 1, :], ALU.mult, ["qC", "carry"], ["hi2"])
                tt(hi_[3], P8i, cr4[:, 0, :], ALU.mult, ["qS", "carry"], ["hi3"])
                tt(hi_[2], hi_[2], hi_[3], ALU.add, ["hi2", "hi3"], ["hi2"])
                tt(pl[0][0][:, :, 0], pl[0][0][:, :, 0], hi_[0], ALU.add, [("pl", 0, 0), "hi0"], [("pl", 0, 0)])
                tt(pl[0][1][:, :, 0], pl[0][1][:, :, 0], hi_[2], ALU.add, [("pl", 0, 1), "hi2"], [("pl", 0, 1)])
                for ri in range(2):
                    V(lambda v, ri=ri: v.tensor_copy(out=Hprev[:, :, ri, 0], in_=cr4[:, ri, :]), ["carry"], ["Hprev"])
            else:
                V(lambda v: v.memset(Hprev[:, :, :, 0], 0.0), [], ["Hprev"])
            V(lambda v: v.tensor_copy(out=Lp[0][:, 0, :], in_=mp["C"][:, :, 8]), ["qC"], ["Lp"])
            V(lambda v: v.tensor_copy(out=Lp[1][:, 0, :], in_=mp["S"][:, :, 8]), ["qS"], ["Lp"])
            for j in range(1, 7):
                a_, b_ = Lp[0][:, j - 1, :], Lp[1][:, j - 1, :]
                tt(hi_[0], a_, a_, ALU.mult, ["Lp"], ["hi0"])
                tt(hi_[1], b_, b_, ALU.mult, ["Lp"], ["hi1"])
                tt(Lp[0][:, j, :], hi_[0], hi_[1], ALU.subtract, ["hi0", "hi1"], ["Lp"])
                V(lambda v, a_=a_, b_=b_, j=j: v.scalar_tensor_tensor(out=Lp[1][:, j, :], in0=a_, scalar=2.0, in1=b_,
                                                                     op0=ALU.mult, op1=ALU.mult), ["Lp"], ["Lp"])
            V(lambda v: v.tensor_scalar(out=Lp[2], in0=Lp[1], scalar1=-1.0, scalar2=None, op0=ALU.mult), ["Lp"], ["Lp2"])
            pp_ = 0
            for j in range(7):
                d = 1 << j
                src, dst = pl[pp_], pl[1 - pp_]
                for ri in range(2):
                    V(lambda v, ri=ri, src=src, dst=dst, d=d: v.tensor_copy(out=dst[ri][:, :, 0:d], in_=src[ri][:, :, 0:d]),
                      [("pl", pp_, ri)], [("pl", 1 - pp_, ri)])
                for pr in range(4):
                    lr_, li_, nli_ = Lp[0][:, j, pr:pr + 1], Lp[1][:, j, pr:pr + 1], Lp[2][:, j, pr:pr + 1]
                    sr, si = src[0][:, pr, :], src[1][:, pr, :]
                    dr, di = dst[0][:, pr, :], dst[1][:, pr, :]
                    rd = [("pl", pp_, 0), ("pl", pp_, 1), "Lp", "Lp2", ("pl", 1 - pp_, 0), ("pl", 1 - pp_, 1)]
                    V(lambda v, sr=sr, dr=dr, lr_=lr_, d=d: v.scalar_tensor_tensor(
                        out=dr[:, d:], in0=sr[:, 0:128 - d], scalar=lr_, in1=sr[:, d:], op0=ALU.mult, op1=ALU.add), rd, [("pl", 1 - pp_, 0)])
                    V(lambda v, si=si, dr=dr, nli_=nli_, d=d: v.scalar_tensor_tensor(
                        out=dr[:, d:], in0=si[:, 0:128 - d], scalar=nli_, in1=dr[:, d:], op0=ALU.mult, op1=ALU.add), rd, [("pl", 1 - pp_, 0)])
                    V(lambda v, si=si, di=di, lr_=lr_, d=d: v.scalar_tensor_tensor(
                        out=di[:, d:], in0=si[:, 0:128 - d], scalar=lr_, in1=si[:, d:], op0=ALU.mult, op1=ALU.add), rd, [("pl", 1 - pp_, 1)])
                    V(lambda v, sr=sr, di=di, li_=li_, d=d: v.scalar_tensor_tensor(
                        out=di[:, d:], in0=sr[:, 0:128 - d], scalar=li_, in1=di[:, d:], op0=ALU.mult, op1=ALU.add), rd, [("pl", 1 - pp_, 1)])
                pp_ = 1 - pp_
            Hf = pl[pp_]
            for ri in range(2):
                V(lambda v, ri=ri, Hf=Hf: v.tensor_copy(out=Hprev[:, :, ri, 1:128], in_=Hf[ri][:, :, 0:127]),
                  [("pl", pp_, ri)], ["Hprev"])
                V(lambda v, ri=ri, Hf=Hf: v.tensor_copy(out=cr4[:, ri, :], in_=Hf[ri][:, :, 127]), [("pl", pp_, ri), "carry"], ["carry"])
            for hb in range(2):
                first = True
                for s in range(8):
                    for tau in range(s + 1):
                        M(lambda t, hb=hb, s=s, tau=tau, first=first: t.matmul(
                            ps[:, 6 + hb, s::8], Kblk[:, tau, :], vbf[:, hb * 512 + s - tau:(hb + 1) * 512:8],
                            start=first, stop=False, skip_group_check=True),
                          ["Kblk", "vbf"], [("ps", 6 + hb)])
                        first = False
                    for pr in range(4):
                        for pln in range(2):
                            M(lambda t, hb=hb, s=s, pr=pr, pln=pln: t.matmul(
                                ps[64 * (pr // 2):64 * (pr // 2) + 64, 6 + hb, s::8], CRh[:, pr, s + 1, pln, :],
                                Hprev[:, pr, pln, 64 * hb:64 * hb + 64], start=False, stop=(s == 7 and pr == 3 and pln == 1),
                                skip_group_check=True),
                              ["CRh", "Hprev"], [("ps", 6 + hb)])
                sl_ = slice(hb * 512, (hb + 1) * 512)
                V(lambda v, hb=hb, sl_=sl_: v.scalar_tensor_tensor(out=yt, in0=vbf[:, sl_], scalar=dvt[:, cc:cc + 1],
                                                                  in1=ps[:, 6 + hb, :], op0=ALU.mult, op1=ALU.add),
                  ["vbf", "dvt", ("ps", 6 + hb)], [("sg", 0)])
                A(lambda a: a.activation(out=gt, in_=yt, func=AF.Square), [("sg", 0)], [("sg", 1)])
                V(lambda v: v.tensor_scalar(out=gt, in0=gt, scalar1=0.044715, scalar2=1.0, op0=ALU.mult, op1=ALU.add), [("sg", 1)], [("sg", 1)])
                tt(gt, gt, yt, ALU.mult, [("sg", 1), ("sg", 0)], [("sg", 1)])
                A(lambda a: a.activation(out=gt, in_=gt, func=AF.Sigmoid, scale=1.5957691216057308), [("sg", 1)], [("sg", 1)])
                V(lambda v, sl_=sl_: v.tensor_tensor(out=B2[:, cc, sl_], in0=yt, in1=gt, op=ALU.mult), [("sg", 0), ("sg", 1)], [("B2", cc)])

        dense(s5_w_in, DC, evac, slots=(0, 1), bank_of=lambda oc: 0, rhs=rhs_B1, rreg=lambda kc: "B1", pre=pre)

        def glu_evac(oc, base):
            A(lambda a: a.activation(out=sg[:, 0, :].bitcast(F32)[:, 0:512], in_=ps[:, base, :], func=AF.Sigmoid,
                                     bias=dvt[:, DC + oc:DC + oc + 1], scale=1.0), [("ps", base), "dvt"], [("sg", 0)])
            tt(B1[:, oc, 0:512], B2[:, oc, 0:512], sg[:, 0, :].bitcast(F32)[:, 0:512], ALU.mult, [("B2", oc), ("sg", 0)], ["B1"])
            A(lambda a: a.activation(out=sg[:, 1, :].bitcast(F32)[:, 0:512], in_=ps[:, base + 1, :], func=AF.Sigmoid,
                                     bias=dvt[:, DC + oc:DC + oc + 1], scale=1.0), [("ps", base + 1), "dvt"], [("sg", 1)])
            tt(B1[:, oc, 512:1024], B2[:, oc, 512:1024], sg[:, 1, :].bitcast(F32)[:, 0:512], ALU.mult, [("B2", oc), ("sg", 1)], ["B1"])
        dense(s5_w_glu, DC, glu_evac, slots=(0, 1, 2), bank_of=lambda oc: 2 * (oc % 4), rhs=rhs_B2, rreg=lambda kc: ("B2", kc))
        dense(s5_w_out, DC, resid_evac, slots=(0, 1, 2), bank_of=lambda oc: 2 * (oc % 4), rhs=rhs_B1, rreg=lambda kc: "B1")

    C1 = 6.28125
    C2 = TWO_PI - 6.28125

    def rope_tables(pi, t0, n, cosT, sinT, Ia, Fa, nm):
        P.add("sp", lambda e: e.dma_start(out=Ia, in_=pos_d[pi, :, t0:t0 + n]), reads=["Ubar"], writes=[nm + "I"], semkey=nm + "pos")
        V(lambda v: v.tensor_copy(out=Fa, in_=Ia), [nm + "I"], [nm + "F"])
        V(lambda v: v.tensor_scalar(out=cosT, in0=Fa, scalar1=INVF, scalar2=None, op0=ALU.mult), [nm + "F", "cst"], [nm + "A"])
        V(lambda v: v.tensor_scalar(out=Ia, in0=cosT, scalar1=1.0 / TWO_PI, scalar2=None, op0=ALU.mult), [nm + "A"], [nm + "I"])
        V(lambda v: v.tensor_copy(out=Fa, in_=Ia), [nm + "I"], [nm + "F"])
        V(lambda v: v.scalar_tensor_tensor(out=cosT, in0=Fa, scalar=-C1, in1=cosT, op0=ALU.mult, op1=ALU.add), [nm + "F", nm + "A"], [nm + "A"])
        V(lambda v: v.scalar_tensor_tensor(out=cosT, in0=Fa, scalar=-C2, in1=cosT, op0=ALU.mult, op1=ALU.add), [nm + "F", nm + "A"], [nm + "A"])
        V(lambda v: v.tensor_scalar(out=cosT, in0=cosT, scalar1=-PI_S, scalar2=PI_S, op0=ALU.max, op1=ALU.min), [nm + "A"], [nm + "A"])
        A(lambda a: a.activation(out=sinT, in_=cosT, func=AF.Sin), [nm + "A"], [nm + "S"])
        A(lambda a: a.activation(out=Fa, in_=cosT, func=AF.Abs), [nm + "A"], [nm + "F"])
        A(lambda a: a.activation(out=cosT, in_=Fa, func=AF.Sin, scale=-1.0, bias=halfpi[:]), [nm + "F", "halfpi"], [nm + "A"])
        V(lambda v: v.tensor_scalar(out=sinT, in0=sinT, scalar1=SGN, scalar2=None, op0=ALU.mult), [nm + "S", "cst"], [nm + "S"])

    att = {}
    dbg_ops = []

    class _Stop(Exception):
        pass

    def att_carve():
        full_barrier()
        uoff[0] = 1024
        att["_"] = 1
        att["K"] = carve(8 * T, BF16, (4, 2, T))
        att["V"] = carve(8 * 256, BF16, (8, 256))
        att["cos"] = carve(T)
        att["sin"] = carve(T)
        att["t"] = [carve(T), carve(T)]
        att["I"] = att["t"][0].bitcast(I32)
        att["F"] = att["t"][1]
        att["E"] = [carve(2 * 512, BF16, (2, 512)), carve(2 * 512, BF16, (2, 512))]
        att["ex"] = [carve(512)] * 2
        att["rd"] = [carve(512)] * 2
        att["cosh"] = carve(128)
        att["sinh"] = carve(128)
        att["Ih"] = carve(128).bitcast(I32)
        att["Fh"] = carve(128)

    def rope_evac(dst, n, cosT, sinT, bx, by, nm, dreg):
        nb = (n + 511) // 512
        for h in range(nb):
            w = min(512, n - h * 512)
            t0, t1 = att["t"]
            V(lambda v, h=h, w=w: v.tensor_tensor(out=t0[:, 0:w], in0=ps[:, bx + h, 0:w], in1=cosT[:, h * 512:h * 512 + w], op=ALU.mult),
              [("ps", bx + h), nm + "A"], ["at0"])
            V(lambda v, h=h, w=w: v.tensor_tensor(out=t1[:, 0:w], in0=ps[:, by + h, 0:w], in1=sinT[:, h * 512:h * 512 + w], op=ALU.mult),
              [("ps", by + h), nm + "S"], ["at1"])
            V(lambda v, h=h, w=w: v.tensor_tensor(out=dst[:, h * 512:h * 512 + w], in0=t0[:, 0:w], in1=t1[:, 0:w], op=ALU.add),
              ["at0", "at1"], [dreg])

    def load_tile(src, s, g):
        P.add("pool", lambda e: e.dma_start(out=wring[:, s, g * 2048:(g + 1) * 2048].rearrange("p (kc c) -> p kc c", c=128), in_=src),
              writes=[("w", s, g)], semkey=("w", s, g))

    def kv_stage(prev):
        if prev:
            tsl, n = slice(T - 128, T), 128
            cosT, sinT = att["cosh"], att["sinh"]
            rope_tables(0, T - 128, 128, cosT, sinT, att["Ih"], att["Fh"], "rh")
            nm = "rh"
        else:
            tsl, n = slice(0, T), T
            cosT, sinT = att["cos"], att["sin"]
            rope_tables(1, 0, T, cosT, sinT, att["I"], att["F"], "ro")
            nm = "ro"
        nb = (n + 511) // 512
        for kv2 in range(8):
            kvh, ver = kv2 // 2, kv2 % 2
            for var in range(2):
                load_tile(w_k2[var, :, kv2 * 128:(kv2 + 1) * 128].rearrange("(kc p) c -> p kc c", p=128), kv2 % 2, var)
            for var in range(2):
                for kc in range(DC):
                    for h in range(nb):
                        w = min(512, n - h * 512)
                        M(lambda t, kv2=kv2, var=var, kc=kc, h=h, w=w: t.matmul(
                            ps[:, 2 * var + h, 0:w], wring[:, kv2 % 2, var * 2048 + kc * 128: var * 2048 + (kc + 1) * 128],
                            B1[:, kc, tsl.start + h * 512: tsl.start + h * 512 + w], start=(kc == 0), stop=(kc == DC - 1)),
                          ["B1", ("w", kv2 % 2, var)], [("ps", 2 * var + h)])
            dst = kh[:, kvh, ver, :] if prev else att["K"][:, kvh, ver, :]
            rope_evac(dst, n, cosT, sinT, 0, 2, nm, "kh" if prev else ("K", kvh))
        P.add("pool", lambda e: e.dma_start(out=wring[:, 2, :].rearrange("p (kc c) -> p kc c", c=256),
                                            in_=w_v.rearrange("(kc p) c -> p kc c", p=128)),
              writes=[("w", 2, 0), ("w", 2, 1)], semkey=("w", 2, 0))
        blocks = [7] if prev else list(range(8))
        for i, blk in enumerate(blocks):
            bank = 4 + (i % 4)
            for kc in range(DC):
                M(lambda t, blk=blk, kc=kc, bank=bank: t.matmul(
                    ps[:, bank, 0:256], B1[:, kc, blk * 128:(blk + 1) * 128], wring[:, 2, kc * 256:(kc + 1) * 256],
                    start=(kc == 0), stop=(kc == DC - 1)),
                  ["B1", ("w", 2, 0), ("w", 2, 1)], [("ps", bank)])
            dstv = vh[:] if prev else att["V"][:, blk, :]
            A(lambda a, bank=bank, dstv=dstv: a.activation(out=dstv, in_=ps[:, bank, 0:256], func=AF.Copy),
              [("ps", bank)], ["vh" if prev else ("V", blk)])

    def attention():
        def q_pre(oc):
            pass
        for j in range(DC):
            for var in range(2):
                load_tile(w_q2[var, :, j * 128:(j + 1) * 128].rearrange("(kc p) c -> p kc c", p=128), j % 3, var)
            for var in range(2):
                for kc in range(DC):
                    for h in range(2):
                        M(lambda t, j=j, var=var, kc=kc, h=h: t.matmul(
                            ps[:, 2 * var + h, :], wring[:, j % 3, var * 2048 + kc * 128: var * 2048 + (kc + 1) * 128],
                            B1[:, kc, h * 512:(h + 1) * 512], start=(kc == 0), stop=(kc == DC - 1)),
                          ["B1", ("w", j % 3, var)], [("ps", 2 * var + h)])
            rope_evac(B2[:, j, :], T, att["cos"], att["sin"], 0, 2, "ro", ("B2", j))
        if debug_stage == "attn":
            dq = dram("dbg_q", [128, DC, T], BF16, kind="ExternalOutput")
            dk = dram("dbg_k", [128, 8, T], BF16, kind="ExternalOutput")
            dbg_ops.append(P.add("sp", lambda e: e.dma_start(out=dq, in_=B2[:]), reads=[("B2", j) for j in range(DC)], semkey="dbgq"))
            dbg_ops.append(P.add("sp", lambda e: e.dma_start(out=dk, in_=att["K"].rearrange("p a b t -> p (a b) t")),
                                 reads=[("K", j) for j in range(4)], semkey="dbgk"))
        it = 0
        for n in range(8):
            for hb in range(8):
                kvh = hb // 2
                pb = 4 * (it % 2)
                E, ex, rd = att["E"][it % 2], att["ex"][it % 2], att["rd"][it % 2]
                it += 1
                for kb in range(2):
                    for hq in range(4):
                        jc, half = 2 * hb + hq // 2, hq % 2
                        if kb == 0 and n == 0:
                            kap, kreg = kh[:, kvh, half, :], "kh"
                        else:
                            kblk = n - 1 + kb
                            kap, kreg = att["K"][:, kvh, half, kblk * 128:(kblk + 1) * 128], ("K", kvh)
                        M(lambda t, kb=kb, hq=hq, jc=jc, kap=kap, pb=pb, n=n: t.matmul(
                            ps[:, pb + kb, hq * 128:(hq + 1) * 128], kap, B2[:, jc, n * 128:(n + 1) * 128],
                            start=True, stop=True, skip_group_check=True),
                          [kreg, ("B2", jc)], [("ps", pb + kb)])
                    A(lambda a, kb=kb, pb=pb, ex=ex: a.activation(out=ex, in_=ps[:, pb + kb, :], func=AF.Exp, scale=0.125),
                      [("ps", pb + kb)], ["ex"])
                    mk = (maskh[:] if n == 0 else maskb[:, 1, :]) if kb == 0 else maskb[:, 0, :]
                    V(lambda v, kb=kb, E=E, ex=ex, mk=mk: v.tensor_tensor(
                        out=E[:, kb, :].rearrange("p (h q) -> p h q", h=4), in0=ex.rearrange("p (h q) -> p h q", h=4),
                        in1=mk.unsqueeze(1).to_broadcast([128, 4, 128]), op=ALU.mult),
                      ["ex", "mask"], [("E", it % 2)])
                for kb in range(2):
                    M(lambda t, kb=kb, E=E, pb=pb: t.matmul(ps[:, pb + 2, :], ones_bf[:], E[:, kb, :], start=(kb == 0), stop=(kb == 1)),
                      [("E", it % 2), "ones"], [("ps", pb + 2)])
                V(lambda v, pb=pb, rd=rd, hb=hb: v.tensor_tensor(
                    out=rd.rearrange("p (h q) -> p h q", h=4), in0=ps[:, pb + 2, :].rearrange("p (h q) -> p h q", h=4),
                    in1=esink[:, 4 * hb:4 * hb + 4].unsqueeze(2).to_broadcast([128, 4, 128]), op=ALU.add),
                  [("ps", pb + 2), "esink"], ["rd"])
                V(lambda v, rd=rd: v.reciprocal(out=rd, in_=rd), ["rd"], ["rd"])
                for hq in range(4):
                    jl, half = hq // 2, hq % 2
                    for kb in range(2):
                        if kb == 0 and n == 0:
                            vap, vreg = vh[:, kvh * 64:(kvh + 1) * 64], "vh"
                        else:
                            vblk = n - 1 + kb
                            vap, vreg = att["V"][:, vblk, kvh * 64:(kvh + 1) * 64], ("V", vblk)
                        M(lambda t, hq=hq, jl=jl, half=half, kb=kb, vap=vap, E=E, pb=pb: t.matmul(
                            ps[64 * half:64 * half + 64, pb + 3, jl * 128:(jl + 1) * 128], vap, E[:, kb, hq * 128:(hq + 1) * 128],
                            start=(kb == 0), stop=(kb == 1), skip_group_check=True),
                          [vreg, ("E", it % 2)], [("ps", pb + 3)])
                for hq in range(4):
                    jl, half = hq // 2, hq % 2
                    V(lambda v, hq=hq, jl=jl, half=half, pb=pb, rd=rd, hb=hb, n=n: v.tensor_tensor(
                        out=B1[64 * half:64 * half + 64, 2 * hb + jl, n * 128:(n + 1) * 128],
                        in0=ps[64 * half:64 * half + 64, pb + 3, jl * 128:(jl + 1) * 128],
                        in1=rd[64 * half:64 * half + 64, hq * 128:(hq + 1) * 128], op=ALU.mult),
                      [("ps", pb + 3), "rd"], ["B1"])
        dense(w_o, DC, resid_evac, slots=(0, 1, 2), bank_of=lambda oc: 2 * (oc % 4), rhs=rhs_B1, rreg=lambda kc: "B1")

    def load_x(src):
        for q in range(4):
            P.add("sp", lambda g, q=q: g.dma_start(out=xT[:, 4 * q:4 * q + 4, :], in_=src[:, 4 * q:4 * q + 4, :]),
                  writes=[("x", dc) for dc in range(4 * q, 4 * q + 4)], semkey=("xl", q))

    aring = B1[:].rearrange("p a b -> p (a b)").bitcast(F32).rearrange("p (s k) -> p s k", s=4)
    cact = B2[:].rearrange("p a b -> p (a b)").bitcast(F32)[:, 0:D]
    P.add("sp", lambda g: g.dma_start(out=cact, in_=c_bc), writes=["cact"], semkey="c")
    P.add("sp", lambda g: g.dma_start(out=bias_sb[:, 0:288], in_=b_ada), writes=["bias"], semkey="b1")
    P.add("sp", lambda g: g.dma_start(out=bias_sb[:, 288:320], in_=b_adakv), writes=["bias"], semkey="b2")
    P.add("sp", lambda g: g.dma_start(out=ng_sb[:, 0:96], in_=norm_g), writes=["ng"], semkey="g1")
    P.add("sp", lambda g: g.dma_start(out=ng_sb[:, 96:112], in_=kvn_g), writes=["ng"], semkey="g2")
    P.add("sp", lambda g: g.dma_start(out=ng_sb[:, 112:128], in_=fin_g), writes=["ng"], semkey="g3")
    P.add("sp", lambda g: g.dma_start(out=flag[:], in_=flag_d), writes=["flag"], semkey="fl")
    P.add("sp", lambda g: g.dma_start(out=cstt[:], in_=cst), writes=["cst"], semkey="cs")
    P.add("sp", lambda g: g.dma_start(out=dvt[:], in_=s5dv), writes=["dvt"], semkey="dv")
    P.add("sp", lambda g: g.dma_start(out=esink[:], in_=sinks_d), writes=["esink"], semkey="sk")
    P.add("pool", lambda g: g.dma_start(out=maskb[:], in_=trimask), writes=["mask"], semkey="mk")
    A(lambda a: a.activation(out=cact, in_=cact, func=AF.Silu), ["cact"], ["cact"])
    A(lambda a: a.activation(out=esink[:], in_=esink[:], func=AF.Exp), ["esink"], ["esink"])
    V(lambda v: v.memset(ones_bf[:], 1.0), [], ["ones"])
    V(lambda v: v.memset(epst[:], EPS), [], ["eps"])
    V(lambda v: v.memset(halfpi[:], math.pi / 2), [], ["halfpi"])
    V(lambda v: v.memset(modraw[:], 0.0), [], [("mod", i) for i in range(320)])
    V(lambda v: v.tensor_scalar(out=maskh[:], in0=maskb[:, 1, :], scalar1=flag[:], scalar2=None, op0=ALU.mult), ["mask", "flag"], ["mask"])
    for layer in range(2):
        for sub in range(3):
            queue_ada(layer, sub)
    queue_ada_kv()
    drain_ada(len(ada_jobs))
    V(lambda v: v.memset(AB[:], 0.0), [("a", s) for s in range(4)] + ["cact"] + [("mod", i) for i in range(320)],
      ["B1", "AB0", "AB1", "AB2"] + [("B2", i) for i in range(DC)])

    def ck(name):
        if debug_stage == name:
            raise _Stop()

    def layer0(prev):
        sfx = "p" if prev else "o"
        load_x(x_prev if prev else x_own)
        prep_AB(0, 0)
        rmsnorm_mod()
        ffn(0, 0)
        ck("ffn1" + sfx)
        prep_AB(DC, 48, gate_scale=1.0)
        rmsnorm_mod()
        if not prev:
            V(lambda v: v.tensor_scalar(out=carry[:], in0=carry[:], scalar1=flag[:], scalar2=None, op0=ALU.mult),
              ["carry", "flag"], ["carry"])
        s5_mixer(prev)
        ck("s5" + sfx)
        prep_AB(2 * DC, 96)
        rmsnorm_mod()
        ffn(0, 1)
        ck("ffn2" + sfx)
        prep_AB(6 * DC, 288, has_gate=False)
        rmsnorm_mod()
        att_carve()
        kv_stage(prev)
        ck("kv" + sfx)

    try:
        layer0(True)
        layer0(False)
        prep_AB(3 * DC, 144)
        rmsnorm_mod()
        ffn(1, 0)
        ck("l1ffn1")
        prep_AB(4 * DC, 192, gate_scale=1.0)
        rmsnorm_mod()
        attention()
        ck("attn")
        prep_AB(5 * DC, 240)
        rmsnorm_mod()
        ffn(1, 1)
        ck("l1ffn2")
        V(lambda v: v.tensor_copy(out=AB[:, 0, :], in_=ng_sb[:, 7 * DC:8 * DC]), ["ng", "AB0"], ["AB0"])
        rmsnorm_mod(out_fp32_dst=xT)
    except _Stop:
        pass

    stores = []
    for q in range(4):
        stores.append(P.add("sp", lambda g, q=q: g.dma_start(out=out_d[:, 4 * q:4 * q + 4, :], in_=xT[:, 4 * q:4 * q + 4, :]),
                            reads=[("x", dc) for dc in range(4 * q, 4 * q + 4)], semkey=("st", q)))
    fin = P.add("sp", None, reads=[("x", dc) for dc in range(DC)])
    fin.deps = list(stores) + dbg_ops

    nsem = 5 + len({op.semkey for e in ENGS for op in P.ops[e] if op.semkey is not None})
    semctx = [nc.semaphore(f"s{i}") for i in range(nsem)]
    sems = [s.__enter__() for s in semctx]
    run = P.emit(nc, None, sems)
    with nc.Block() as block:
        @block.tensor
        def _(e):
            run("pe", e)

        @block.scalar
        def _(e):
            run("act", e)

        @block.vector
        def _(e):
            run("dve", e)

        @block.gpsimd
        def _(e):
            run("pool", e)

        @block.sync
        def _(e):
            run("sp", e)
    for s in reversed(semctx):
        s.__exit__(None, None, None)
    for t in reversed(ctx):
        t.__exit__(None, None, None)
    return nc


def _fm(a):
    return np.ascontiguousarray(a.T.reshape(DC, 128, a.shape[0]).transpose(1, 0, 2))


def _vec(v):
    return np.ascontiguousarray(np.asarray(v, np.float32).reshape(-1, 128).T)


def _consts():
    cst = np.zeros((128, 80), np.float32)
    part = np.arange(128)
    cst[:, 0:8] = np.arange(7, -1, -1, dtype=np.float32)[None, :]
    cst[:, 8:17] = np.arange(9, dtype=np.float32)[None, :]
    for q in range(4):
        cst[:, 17 + q] = ((part // 16) % 4 == q)
    for pr in range(4):
        for pq in range(2):
            for m in range(2):
                cst[:, 21 + pr * 4 + pq * 2 + m] = (pq == pr % 2) & (part // 64 == m)
        for pr2 in range(4):
            for m in range(2):
                cst[:, 37 + pr * 8 + pr2 * 2 + m] = (pr2 == pr) & (part // 64 == m)
    inv_freq = (1.0 / (10000.0 ** (np.arange(0, 64, 2, dtype=np.float32) / np.float32(64)))).astype(np.float32)
    cst[:, 69] = inv_freq[(part % 64) % 32]
    cst[:, 70] = np.where((part % 64) < 32, -1.0, 1.0)
    tk = np.arange(128)[:, None]
    tq = np.arange(128)[None, :]
    tri = np.stack([(tk <= tq), (tk > tq)], axis=1).astype(np.float32)
    return cst, np.ascontiguousarray(tri)


def _s5_layouts(inp):
    a_re = np.asarray(inp["s5_a_re"][0], np.float32)
    a_im = np.asarray(inp["s5_a_im"][0], np.float32)
    ldt = np.asarray(inp["s5_log_dt"][0], np.float32)
    b_re = np.asarray(inp["s5_b_re"][0], np.float32)
    b_im = np.asarray(inp["s5_b_im"][0], np.float32)
    c_re = np.asarray(inp["s5_c_re"][0], np.float32)
    c_im = np.asarray(inp["s5_c_im"][0], np.float32)

    def cm_gp(a):
        return np.repeat(a.reshape(16, 8, 64), 16, axis=1)

    def cm_b(b):
        return b.reshape(16, 8, 64, 16).transpose(0, 1, 3, 2).reshape(16, 128, 64)
    ldt_gp = np.repeat(ldt[:, None], 64, axis=1)
    s5cm = np.concatenate([cm_gp(a_re), cm_gp(a_im), cm_gp(ldt_gp), cm_b(b_re), cm_b(b_im)], axis=2)

    def pm_gp(a):
        return a.reshape(16, 4, 2, 64).transpose(0, 2, 3, 1).reshape(16, 128, 4)

    def pm_c(c):
        return c.reshape(16, 4, 2, 16, 64).transpose(0, 2, 4, 1, 3).reshape(16, 128, 64)

    def pm_b(b):
        return b.reshape(16, 4, 2, 64, 16).transpose(0, 2, 3, 1, 4).reshape(16, 128, 64)
    s5pm = np.concatenate([pm_gp(a_re), pm_gp(a_im), pm_gp(ldt_gp), pm_c(c_re), pm_c(c_im), pm_b(b_re), pm_b(b_im)], axis=2)
    return np.ascontiguousarray(s5cm, np.float32), np.ascontiguousarray(s5pm, np.float32)


def make_in_maps(inp, cores=range(8)):
    x = np.asarray(inp["x"], np.float32)
    c = np.asarray(inp["c"], np.float32)
    pos = np.asarray(inp["positions"], np.int32)
    w_adaT = np.ascontiguousarray(np.asarray(inp["w_ada"]).transpose(0, 2, 1)).reshape(2, 144, 128, D)
    w_adakvT = np.ascontiguousarray(np.asarray(inp["w_ada_kv"]).T).reshape(32, 128, D)
    cst, tri = _consts()
    s5cm, s5pm = _s5_layouts(inp)
    wq = np.asarray(inp["attn_w_q"][0], np.float32)
    wq_sw = wq.reshape(D, 32, 2, 32)[:, :, ::-1, :].reshape(D, D)
    wkv = np.asarray(inp["w_kv"], np.float32)
    wk = wkv[:, :256].reshape(D, 4, 64)
    wk_sw = wk.reshape(D, 4, 2, 32)[:, :, ::-1, :].reshape(D, 4, 64)

    def dup(w):
        o = np.zeros((D, 4, 2, 2, 64), np.float32)
        o[:, :, 0, 0, :] = w
        o[:, :, 1, 1, :] = w
        return o.reshape(D, 1024)
    shared = {
        "norm_g": _vec(np.asarray(inp["norm_g"]).reshape(-1)),
        "fin_g": _vec(inp["final_norm_g"]),
        "kvn_g": _vec(inp["kv_norm_g"]),
        "b_ada": _vec(np.asarray(inp["b_ada"]).reshape(-1)),
        "b_adakv": _vec(inp["b_ada_kv"]),
        "w_adaT": w_adaT,
        "w_adakvT": w_adakvT,
        "w_ff_in": np.asarray(inp["w_ff_in"], np.float32),
        "w_ff_out": np.asarray(inp["w_ff_out"], np.float32),
        "s5_w_in": np.ascontiguousarray(inp["s5_w_in"][0], np.float32),
        "s5_w_glu": np.ascontiguousarray(inp["s5_w_glu"][0], np.float32),
        "s5_w_out": np.ascontiguousarray(inp["s5_w_out"][0], np.float32),
        "s5cm": s5cm,
        "s5pm": s5pm,
        "s5dv": np.concatenate([_vec(np.asarray(inp["s5_d"][0]).reshape(-1)), _vec(inp["s5_b_glu"][0])], axis=1),
        "cst": cst,
        "trimask": tri,
        "w_q2": np.ascontiguousarray(np.stack([wq, wq_sw])),
        "w_k2": np.ascontiguousarray(np.stack([dup(wk), dup(wk_sw)])),
        "w_v": np.ascontiguousarray(wkv[:, 256:]),
        "w_o": np.ascontiguousarray(inp["attn_w_o"][0], np.float32),
        "sinks": np.ascontiguousarray(np.broadcast_to(np.asarray(inp["attn_sinks"][0], np.float32)[None, :], (128, 32))),
    }
    maps = []
    for i in cores:
        b, hf = i // 2, i % 2
        m = dict(shared)
        m["x_own"] = _fm(x[b, hf * T:(hf + 1) * T])
        m["x_prev"] = _fm(x[b, 0:T]) if hf == 1 else np.zeros((128, DC, T), np.float32)
        m["flag"] = np.full((128, 1), float(hf), np.float32)
        m["c_bc"] = np.ascontiguousarray(np.broadcast_to(c[b][None, :], (128, D)))
        pp = np.stack([pos[b, 0:T], pos[b, hf * T:(hf + 1) * T]])
        m["pos"] = np.ascontiguousarray(np.broadcast_to(pp[:, None, :], (2, 128, T))).astype(np.int32)
        maps.append(m)
    return maps


def kernel(**inp):
    nc = build_program()
    maps = make_in_maps(inp)
    res = run_bass_kernel_spmd(nc, maps, core_ids=list(range(8)))
    out = np.zeros((4, 2048, D), np.float32)
    for i in range(8):
        b, hf = i // 2, i % 2
        o = res.results[i]["out"]
        out[b, hf * T:(hf + 1) * T] = o.transpose(1, 0, 2).reshape(D, T).T
    return out
```
